# Optimizing a Trainium2 kernel written in Bass

```python
import jax, jax.numpy as jnp
from jax import lax
import numpy as np

D_MODEL = 1024
BATCH = 8
SEQ = 4096
DEPTH = 4

CONV_WIDTH = 3
CONV_DIM = D_MODEL
CONV_GROUPS = 8
SGU_DIM = D_MODEL
SGU_HEADS = 8
SGU_HEAD_DIM = SGU_DIM // SGU_HEADS
CHUNK = 128
N_BRANCHES = 2
D_FF = 4 * D_MODEL
IN_COLS = 3 * CONV_DIM + 2 * SGU_DIM + N_BRANCHES * D_MODEL
EPS = 1e-6

kernel_name = "hybrid_shortconv_chunked_sgu_block"

SPLIT_POINTS = [
    CONV_DIM,
    2 * CONV_DIM,
    3 * CONV_DIM,
    3 * CONV_DIM + SGU_DIM,
    3 * CONV_DIM + 2 * SGU_DIM,
    3 * CONV_DIM + 2 * SGU_DIM + D_MODEL,
]


def rms_norm(x, g):
    xf = x.astype(jnp.float32)
    y = xf * lax.rsqrt(jnp.mean(xf * xf, axis=-1, keepdims=True) + EPS)
    return (y * g.astype(jnp.float32)).astype(x.dtype)


def layer_norm(x, g, b):
    xf = x.astype(jnp.float32)
    mu = jnp.mean(xf, axis=-1, keepdims=True)
    xc = xf - mu
    var = jnp.mean(xc * xc, axis=-1, keepdims=True)
    y = xc * lax.rsqrt(var + EPS) * g.astype(jnp.float32) + b.astype(jnp.float32)
    return y.astype(x.dtype)


def causal_depthwise_conv(z, w):
    seq = z.shape[1]
    zp = jnp.pad(z, ((0, 0), (CONV_WIDTH - 1, 0), (0, 0)))
    y = w[0] * zp[:, 0:seq]
    for k in range(1, CONV_WIDTH):
        y = y + w[k] * zp[:, k:k + seq]
    return y


def chunked_spatial_gating(u, v, w_s, b_s, ln_g, ln_b):
    bsz, seq, _ = v.shape
    n_chunks = seq // CHUNK
    vn = layer_norm(v, ln_g, ln_b).reshape(bsz, n_chunks, CHUNK, SGU_HEADS, SGU_HEAD_DIM)
    causal_mask = jnp.tril(jnp.ones((CHUNK, CHUNK), dtype=w_s.dtype))
    mixed = jnp.einsum('hts,bnshd->bnthd', w_s * causal_mask, vn)
    mixed = mixed + jnp.transpose(b_s)[None, None, :, :, None]
    return u * mixed.reshape(bsz, seq, SGU_DIM)


def setup_inputs(seed: int = 0) -> dict:
    key = jax.random.key(seed)
    ks = jax.random.split(key, 14)
    f32 = jnp.float32
    x = jax.random.normal(ks[0], (BATCH, SEQ, D_MODEL), f32)
    norm_mix = 1.0 + 0.02 * jax.random.normal(ks[1], (DEPTH, D_MODEL), f32)
    w_in = jax.random.normal(ks[2], (DEPTH, D_MODEL, IN_COLS), f32) * D_MODEL ** -0.5
    conv_w = jax.random.normal(ks[3], (DEPTH, CONV_WIDTH, CONV_DIM), f32) * CONV_WIDTH ** -0.5
    sgu_w = jax.random.normal(ks[4], (DEPTH, SGU_HEADS, CHUNK, CHUNK), f32) * CHUNK ** -0.5
    sgu_b = 1.0 + 0.02 * jax.random.normal(ks[5], (DEPTH, SGU_HEADS, CHUNK), f32)
    sgu_ln_g = 1.0 + 0.02 * jax.random.normal(ks[6], (DEPTH, SGU_DIM), f32)
    sgu_ln_b = 0.02 * jax.random.normal(ks[7], (DEPTH, SGU_DIM), f32)
    w_out = jax.random.normal(ks[8], (DEPTH, D_MODEL, D_MODEL), f32) * D_MODEL ** -0.5
    norm_mlp = 1.0 + 0.02 * jax.random.normal(ks[9], (DEPTH, D_MODEL), f32)
    w_ff1 = jax.random.normal(ks[10], (DEPTH, D_MODEL, D_FF), f32) * D_MODEL ** -0.5
    w_ff2 = jax.random.normal(ks[11], (DEPTH, D_FF, D_MODEL), f32) * D_FF ** -0.5
    final_norm = 1.0 + 0.02 * jax.random.normal(ks[12], (D_MODEL,), f32)
    return {
        "x": x, "norm_mix": norm_mix, "w_in": w_in, "conv_w": conv_w,
        "sgu_w": sgu_w, "sgu_b": sgu_b, "sgu_ln_g": sgu_ln_g, "sgu_ln_b": sgu_ln_b,
        "w_out": w_out, "norm_mlp": norm_mlp, "w_ff1": w_ff1, "w_ff2": w_ff2,
        "final_norm": final_norm,
    }


def reference(x, norm_mix, w_in, conv_w, sgu_w, sgu_b, sgu_ln_g, sgu_ln_b,
              w_out, norm_mlp, w_ff1, w_ff2, final_norm):
    for l in range(DEPTH):
        h = rms_norm(x, norm_mix[l])
        proj = jnp.einsum('bsd,dc->bsc', h, w_in[l])
        c_gate, b_gate, a_in, u, v, g_a, g_b = jnp.split(proj, SPLIT_POINTS, axis=-1)
        y_a = b_gate * causal_depthwise_conv(c_gate * a_in, conv_w[l])
        y_b = chunked_spatial_gating(jax.nn.gelu(u), jax.nn.gelu(v), sgu_w[l], sgu_b[l],
                                     sgu_ln_g[l], sgu_ln_b[l])
        merged = jax.nn.sigmoid(g_a) * y_a + jax.nn.sigmoid(g_b) * y_b
        x = x + jnp.einsum('bsd,de->bse', merged, w_out[l])
        h = rms_norm(x, norm_mlp[l])
        hid = jnp.square(jax.nn.relu(jnp.einsum('bsd,df->bsf', h, w_ff1[l])))
        x = x + jnp.einsum('bsf,fd->bsd', hid, w_ff2[l])
    return rms_norm(x, final_norm)
```

```python
import numpy as np
from contextlib import ExitStack
import concourse.bass as bass
import concourse.mybir as mybir
from concourse.bass_utils import run_bass_kernel_spmd

F32 = mybir.dt.float32
BF16 = mybir.dt.bfloat16
AF = mybir.ActivationFunctionType
ALU = mybir.AluOpType

D = 1024
SEQ = 4096
NCORES = 8
EPS = 1e-6
C1 = 1.5957691216057308
C2S = 0.21145921592173913
HT = 512
LOADS = ([(0, 4), (4, 4)] + [(8 + 6 * j, 6) for j in range(8)] + [(56, 4), (60, 4)]
         + [(64 + 4 * i, 4) for i in range(8)] + [(96 + 4 * e, 4) for e in range(8)])
NLD = len(LOADS)
NPP = 232


class Op:
    __slots__ = ("eng", "fn", "deps", "ms", "is_dma", "dsem", "dval", "needed")

    def __init__(self, eng, fn, deps, is_dma):
        self.eng = eng
        self.fn = fn
        self.deps = deps
        self.is_dma = is_dma
        self.ms = 0
        self.dsem = None
        self.dval = 0
        self.needed = False


class Buf:
    __slots__ = ("w", "r")

    def __init__(self):
        self.w = None
        self.r = {}


class Sched:
    ENGS = ("pe", "act", "dve", "pool", "sp")

    def __init__(self):
        self.q = {e: [] for e in self.ENGS}
        self.ndma = {"pool": 0, "sp": 0}
        self.dma_ops = {"pool": [], "sp": []}

    def op(self, eng, fn, reads=(), writes=(), extra=(), dma=False):
        deps = []

        def add(d, raw):
            if d is None:
                return
            if d.is_dma or dma:
                deps.append(d)
            elif d.eng != eng:
                deps.append(d)
            elif eng != "pe":
                deps.append(d)
        for b in reads:
            add(b.w, True)
        for b in writes:
            add(b.w, False)
            for r in b.r.values():
                add(r, False)
        for d in extra:
            if d is not None:
                deps.append(d)
        o = Op(eng, fn, deps, dma)
        if dma:
            lst = self.dma_ops[eng]
            if len(lst) >= 8:
                o.deps.append(lst[-8])
            lst.append(o)
        for d in o.deps:
            d.needed = True
        key = ("dma", id(o)) if dma else eng
        for b in reads:
            b.r[key] = o
        for b in writes:
            b.w = o
            b.r = {}
        self.q[eng].append(o)
        return o


def _build(DEPTH, NT, NH, RB, NXB, NTMP=6):
    T = NH * HT
    NTC = HT // 128
    nc = bass.Bass("TRN2", target_bir_lowering=False)
    xT = nc.dram_tensor("xT", [NT * NH, 128, 8 * HT], F32, kind="ExternalInput").ap()
    wst = nc.dram_tensor("wst", [DEPTH, 128, 128 * 1024], F32, kind="ExternalInput").ap()
    ppd = nc.dram_tensor("pp", [128, NPP], F32, kind="ExternalInput").ap()
    sgwd = nc.dram_tensor("sgw", [DEPTH, 128, 1024], F32, kind="ExternalInput").ap()
    sgbd = nc.dram_tensor("sgb", [DEPTH, 128, 1024], F32, kind="ExternalInput").ap()
    outT = nc.dram_tensor("outT", [NT * NH, 128, 8 * HT], F32, kind="ExternalOutput").ap()

    S = Sched()
    with ExitStack() as es:
        def sb(name, shape, dt):
            return es.enter_context(nc.sbuf_tensor(name, shape, dt))
        xs = sb("xs", [128, NXB * 8 * T], F32)
        hh = sb("hh", [128, NH * 8 * HT], BF16)
        RR = sb("RR", [128, NH * 32 * HT], BF16)
        tmpt = sb("tmpt", [128, NTMP * 516], F32)
        ygt = sb("ygt", [128, NH * 2 * HT], F32)
        gvt = sb("gvt", [128, 4 * 1024], F32)
        ring = sb("ring", [128, RB * 1024], BF16)
        ones_bf = sb("ones_bf", [128, 128], BF16)
        ones1 = sb("ones1", [128, 128], BF16)
        pp = sb("pps", [128, NPP], F32)
        Et = sb("Et", [128, 2 * 1024], F32)
        bsbc = sb("bsbc", [128, 1024], F32)
        wmt = sb("wmt", [128, 2 * 1024], BF16)
        carry = sb("carry", [128, DEPTH * 8 * 2], F32)
        scr = sb("scr", [128, 4], F32)
        stt = sb("stt", [128, 4 * 12], F32)
        mvt = sb("mvt", [128, 4 * 4], F32)
        psums = [es.enter_context(nc.psum_tensor(f"ps{i}", [128, 512], F32)) for i in range(8)]
        psb = [Buf() for _ in range(8)]
        prog = {e: es.enter_context(nc.semaphore("sem_" + e)) for e in ("pe", "act", "dve", "pool")}
        dsems = {q: [es.enter_context(nc.semaphore(f"dma_{q}{i}")) for i in range(8)] for q in ("pool", "sp")}

        xb = [[[Buf() for _ in range(8)] for _ in range(NH)] for _ in range(NXB)]
        hb = [[Buf() for _ in range(8)] for _ in range(NH)]
        Rb = [[Buf() for _ in range(32)] for _ in range(NH)]
        tmpb = [Buf() for _ in range(NTMP)]
        ygb = [[Buf() for _ in range(2)] for _ in range(NH)]
        gvb = [Buf() for _ in range(4)]
        stb = [Buf() for _ in range(4)]
        onesb, ppb, bsbcb = Buf(), Buf(), Buf()
        Ebs, wmtbs = [Buf(), Buf()], [Buf(), Buf()]
        carryb = [[Buf() for _ in range(8)] for _ in range(DEPTH)]

        def x_ap(xbi, hf, c):
            o = ((xbi * NH + hf) * 8 + c) * HT
            return xs[:, o:o + HT]

        def h_ap(hf, k, lo=0, n=HT):
            o = (hf * 8 + k) * HT + lo
            return hh[:, o:o + n]

        def R_ap(hf, slot, n=HT):
            o = (hf * 32 + slot) * HT
            return RR[:, o:o + n]

        def yg_ap(hf, i):
            o = (hf * 2 + i) * HT
            return ygt[:, o:o + HT]

        cnt = {"tmp": 0, "bank": 0, "gv": 0}

        def tmp():
            i = cnt["tmp"] % NTMP
            cnt["tmp"] += 1
            return tmpt[:, i * 516:(i + 1) * 516], tmpb[i]

        def bank():
            i = cnt["bank"] % 8
            cnt["bank"] += 1
            return psums[i], psb[i]

        tot_loads = NT * DEPTH * NLD
        place = []
        free_dep = []
        live = []
        pos = 0
        for gl in range(tot_loads):
            nb = LOADS[gl % NLD][1]
            if pos + nb > RB:
                pos = 0
            st, en = pos, pos + nb
            dep = -1
            keep = []
            for (g2, s2, e2) in live:
                if s2 < en and st < e2:
                    dep = max(dep, g2)
                else:
                    keep.append((g2, s2, e2))
            live = keep + [(gl, st, en)]
            place.append(st)
            free_dep.append(dep)
            pos = en
        wstate = {"next": 0}
        load_op = [None] * tot_loads
        rel_op = [None] * tot_loads

        def issue_loads():
            while wstate["next"] < tot_loads:
                gl = wstate["next"]
                fd = free_dep[gl]
                if fd >= 0 and rel_op[fd] is None:
                    return
                l = (gl // NLD) % DEPTH
                b0, nb = LOADS[gl % NLD]
                dst = ring[:, place[gl] * 1024:(place[gl] + nb) * 1024]
                src = wst[l, :, b0 * 1024:(b0 + nb) * 1024]
                load_op[gl] = S.op("pool", lambda e, dst=dst, src=src: e.dma_start(out=dst, in_=src),
                                   extra=[rel_op[fd]] if fd >= 0 else [], dma=True)
                wstate["next"] += 1

        def wblk(gl, b):
            assert load_op[gl] is not None, "weight ring too small"
            o = (place[gl] + b) * 1024
            return ring[:, o:o + 1024]

        def release(gl, op):
            rel_op[gl] = op
            issue_loads()

        def mm(out, lhsT, rhs, start, stop):
            return lambda e: e.matmul(out, lhsT=lhsT, rhs=rhs, start=start, stop=stop)

        S.op("dve", lambda e: e.memset(ones_bf[:], 1.0 / 1024.0), writes=[onesb])
        S.op("dve", lambda e: e.memset(ones1[:], 1.0), writes=[onesb])
        scrb, scrob = Buf(), Buf()
        S.op("dve", lambda e: e.memset(scr[:], 1.0), writes=[scrb])

        def act_preload(func):
            S.op("act", lambda e: e.activation(out=scr[:, 1:2], in_=scr[:, 0:1], func=func), reads=[scrb], writes=[scrob])

        S.op("dve", lambda e: e.memset(carry[:], 0.0), writes=[b for l in carryb for b in l])
        S.op("sp", lambda e: e.dma_start(out=pp[:], in_=ppd), writes=[ppb], dma=True)
        S.op("dve", lambda e: e.tensor_scalar_mul(out=pp[:, 72:200], in0=pp[:, 72:200], scalar1=0.5),
             reads=[ppb], writes=[ppb])

        def load_x(tile):
            xbi = tile % NXB
            for hf in range(NH):
                for c in range(8):
                    dst = x_ap(xbi, hf, c)
                    src = xT[c * 128:(c + 1) * 128, tile * T + hf * HT:tile * T + (hf + 1) * HT]
                    S.op("sp", lambda e, dst=dst, src=src: e.dma_start(out=dst, in_=src),
                         writes=[xb[xbi][hf][c]], dma=True)

        def layer_params_load(l, pi):
            wm = wmt[:, pi * 1024:(pi + 1) * 1024]
            S.op("sp", lambda e: e.dma_start(out=bsbc[:], in_=sgbd[l]), writes=[bsbcb], dma=True)
            S.op("pool", lambda e: e.dma_start(out=wm, in_=sgwd[l]), writes=[wmtbs[pi]], dma=True)
            S.op("pool", lambda e: e.affine_select(out=wm, in_=wm, pattern=[[0, 8], [1, 128]],
                                                     compare_op=ALU.is_ge, fill=0.0, base=0,
                                                     channel_multiplier=-1),
                 reads=[wmtbs[pi]], writes=[wmtbs[pi]])

        def layer_params_compute(l, pi):
            wm = wmt[:, pi * 1024:(pi + 1) * 1024]
            Ep = Et[:, pi * 1024:(pi + 1) * 1024]
            for hlf in range(2):
                ps, pb = bank()
                S.op("pe", mm(ps[:], ones1[:], wm[:, hlf * 512:(hlf + 1) * 512], True, True),
                     reads=[wmtbs[pi], onesb], writes=[pb])
                for jj in range(4):
                    j = hlf * 4 + jj
                    lbc = pp[:, 200 + l * 8 + j:200 + l * 8 + j + 1]
                    S.op("dve", lambda e, ps=ps, jj=jj, j=j, lbc=lbc: e.scalar_tensor_tensor(
                        out=Ep[:, j * 128:(j + 1) * 128], in0=ps[:, jj * 128:(jj + 1) * 128], scalar=lbc,
                        in1=bsbc[:, j * 128:(j + 1) * 128], op0=ALU.mult, op1=ALU.add),
                        reads=[pb, ppb, bsbcb], writes=[Ebs[pi]])
            S.op("dve", lambda e: e.tensor_scalar_mul(out=Ep, in0=Ep, scalar1=0.5), reads=[Ebs[pi]], writes=[Ebs[pi]])

        def xr_ap(hf, c):
            o = (hf * 32 + 16 + 2 * c) * HT
            return RR[:, o:o + 2 * HT].bitcast(F32)

        def xsrc(xbi, hf, c, fromR):
            if fromR:
                return xr_ap(hf, c), [Rb[hf][16 + 2 * c], Rb[hf][17 + 2 * c]]
            return x_ap(xbi, hf, c), [xb[xbi][hf][c]]

        def square_chunk(xbi, hf, c, fromR=False):
            src, sbufs = xsrc(xbi, hf, c, fromR)
            S.op("act", lambda e: e.activation(out=h_ap(hf, c), in_=src, func=AF.Square),
                 reads=sbufs, writes=[hb[hf][c]])

        def copy_x_from_R(xbi, hf):
            for c in range(8):
                src, sbufs = xsrc(xbi, hf, c, True)
                S.op("pool", lambda e, c=c, src=src: e.tensor_copy(out=x_ap(xbi, hf, c), in_=src),
                     reads=sbufs, writes=[xb[xbi][hf][c]])

        def norm_pre(xbi, hf, squares=False, fromR=False):
            if squares:
                for c in range(8):
                    square_chunk(xbi, hf, c, fromR)
            ps, pb = bank()
            for c in range(8):
                S.op("pe", mm(ps[:], ones_bf[:], h_ap(hf, c), c == 0, c == 7),
                     reads=[hb[hf][c], onesb], writes=[pb])
            return ps, pb

        def norm_post(xbi, hf, pre, gcol, final=False, tile=0, fromR=False):
            ps, pb = pre
            t, tb = tmp()
            S.op("act", lambda e: e.activation(out=t[:, 0:HT], in_=ps[:], func=AF.Sqrt, bias=EPS),
                 reads=[pb], writes=[tb])
            if not final and gcol < 32:
                act_preload(AF.Gelu_apprx_tanh)
            S.op("dve", lambda e: e.reciprocal(out=t[:, 0:HT], in_=t[:, 0:HT]), reads=[tb], writes=[tb])
            stores = []
            for c in range(8):
                g = pp[:, gcol + c:gcol + c + 1]
                if not final:
                    src, sbufs = xsrc(xbi, hf, c, fromR)
                    S.op("dve", lambda e, c=c, g=g, src=src: e.scalar_tensor_tensor(
                        out=h_ap(hf, c), in0=src, scalar=g, in1=t[:, 0:HT],
                        op0=ALU.mult, op1=ALU.mult),
                        reads=sbufs + [tb, ppb], writes=[hb[hf][c]])
                else:
                    o32 = RR[:, (hf * 32 + 2 * c) * HT:(hf * 32 + 2 * c + 2) * HT].bitcast(F32)
                    S.op("dve", lambda e, c=c, g=g, o32=o32: e.scalar_tensor_tensor(
                        out=o32, in0=x_ap(xbi, hf, c), scalar=g, in1=t[:, 0:HT],
                        op0=ALU.mult, op1=ALU.mult),
                        reads=[xb[xbi][hf][c], tb, ppb], writes=[Rb[hf][2 * c], Rb[hf][2 * c + 1]])
            if final:
                src = RR[:, hf * 32 * HT:hf * 32 * HT + 16 * HT].bitcast(F32)
                dst = outT[tile * NH + hf]
                stores.append(S.op("sp", lambda e, src=src, dst=dst: e.dma_start(out=dst, in_=src),
                                   reads=[Rb[hf][i] for i in range(16)], dma=True))
            return stores

        def gelu_chain(ps, pb, out_ap, outb, extra_reads=()):
            S.op("act", lambda e: e.activation(out=out_ap, in_=ps[:], func=AF.Gelu_apprx_tanh),
                 reads=[pb] + list(extra_reads), writes=[outb])

        def v_units(gbase, hf, tc):
            gv0, gv1 = gbase + 0, gbase + 1
            gv = gvt[:, tc * 1024:(tc + 1) * 1024]
            gb = gvb[tc]
            for ch in range(2):
                gl = (gv0, gv1)[ch]
                ps, pb = bank()
                for k in range(8):
                    w = wblk(gl, k // 2)[:, (k % 2) * 512:(k % 2) * 512 + 512]
                    o = S.op("pe", mm(ps[:], h_ap(hf, k, tc * 128, 128), w, k == 0, k == 7),
                             reads=[hb[hf][k]], writes=[pb], extra=[load_op[gl]])
                if hf == NH - 1 and tc == NTC - 1:
                    release(gl, o)
                gelu_chain(ps, pb, gv[:, ch * 512:(ch + 1) * 512], gb)
            st = stt[:, tc * 12:(tc + 1) * 12]
            mv = mvt[:, tc * 4:(tc + 1) * 4]
            sb_ = stb[tc]
            S.op("dve", lambda e, st=st, gv=gv: e.bn_stats(out=st[:, 0:6], in_=gv[:, 0:512]),
                 reads=[gb], writes=[sb_])
            S.op("dve", lambda e, st=st, gv=gv: e.bn_stats(out=st[:, 6:12], in_=gv[:, 512:1024]),
                 reads=[gb], writes=[sb_])
            S.op("dve", lambda e, st=st, mv=mv: e.bn_aggr(out=mv[:, 0:2], in_=st[:, 0:12]),
                 reads=[sb_], writes=[sb_])

        def v_finish(hf):
            for tc in range(NTC):
                mv = mvt[:, tc * 4:(tc + 1) * 4]
                S.op("act", lambda e, mv=mv: e.activation(out=mv[:, 2:3], in_=mv[:, 1:2], func=AF.Sqrt, bias=EPS),
                     reads=[stb[tc]], writes=[stb[tc]])
            act_preload(AF.Gelu_apprx_tanh)
            for tc in range(NTC):
                mv = mvt[:, tc * 4:(tc + 1) * 4]
                S.op("dve", lambda e, mv=mv: e.reciprocal(out=mv[:, 2:3], in_=mv[:, 2:3]), reads=[stb[tc]], writes=[stb[tc]])
                S.op("dve", lambda e, mv=mv: e.scalar_tensor_tensor(out=mv[:, 3:4], in0=mv[:, 0:1], scalar=-1.0,
                                                                   in1=mv[:, 2:3], op0=ALU.mult, op1=ALU.mult),
                     reads=[stb[tc]], writes=[stb[tc]])
            for tc in range(NTC):
                mv = mvt[:, tc * 4:(tc + 1) * 4]
                gv = gvt[:, tc * 1024:(tc + 1) * 1024]
                vn = RR[:, (hf * 32 + 2 * tc) * HT:(hf * 32 + 2 * tc + 2) * HT]
                S.op("act", lambda e, mv=mv, gv=gv, vn=vn: e.activation(out=vn, in_=gv[:, :], func=AF.Identity,
                                                                        scale=mv[:, 2:3], bias=mv[:, 3:4]),
                     reads=[gvb[tc], stb[tc]], writes=[Rb[hf][2 * tc], Rb[hf][2 * tc + 1]])

        def proj_unit(gl, blk, hf, last):
            ps, pb = bank()
            w = wblk(gl, blk)
            for k in range(8):
                o = S.op("pe", mm(ps[:], w[:, k * 128:(k + 1) * 128], h_ap(hf, k), k == 0, k == 7),
                         reads=[hb[hf][k]], writes=[pb], extra=[load_op[gl]])
            if last:
                release(gl, o)
            return ps, pb

        def mix_j(l, gl, j, xbi, pi):
            cw = 72 + l * 24
            res = {}
            for blk in range(4):
                for hf in range(NH):
                    res[(blk, hf)] = proj_unit(gl, blk, hf, False)
                for hf in range(NH):
                    ps, pb = res[(blk, hf)]
                    if blk == 0:
                        ta, tab = tmp()
                        S.op("act", lambda e, ta=ta, ps=ps: e.activation(out=ta[:, 0:HT], in_=ps[:], func=AF.Copy),
                             reads=[pb], writes=[tab])
                        res[("a", hf)] = (ta, tab)
                    elif blk == 1:
                        ta, tab = res[("a", hf)]
                        z, zb = tmp()
                        y, yb = yg_ap(hf, 0), ygb[hf][0]
                        cb = carryb[l][j]
                        cap = carry[:, (l * 8 + j) * 2:(l * 8 + j) * 2 + 2]
                        S.op("dve", lambda e, z=z, cap=cap: e.tensor_copy(out=z[:, 0:2], in_=cap), reads=[cb], writes=[zb])
                        S.op("dve", lambda e, z=z, ps=ps, ta=ta: e.tensor_tensor(out=z[:, 2:2 + HT], in0=ps[:], in1=ta[:, 0:HT], op=ALU.mult),
                             reads=[pb, tab], writes=[zb])
                        S.op("dve", lambda e, z=z, cap=cap: e.tensor_copy(out=cap, in_=z[:, HT:HT + 2]), reads=[zb], writes=[cb])
                        w0 = pp[:, cw + 0 * 8 + j:cw + 0 * 8 + j + 1]
                        w1 = pp[:, cw + 1 * 8 + j:cw + 1 * 8 + j + 1]
                        w2 = pp[:, cw + 2 * 8 + j:cw + 2 * 8 + j + 1]
                        S.op("act", lambda e, z=z, y=y, w2=w2: e.activation(out=y[:, 0:HT], in_=z[:, 2:2 + HT], func=AF.Identity, scale=w2),
                             reads=[zb, ppb], writes=[yb])
                        S.op("dve", lambda e, z=z, y=y, w1=w1: e.scalar_tensor_tensor(out=y[:, 0:HT], in0=z[:, 1:1 + HT], scalar=w1, in1=y[:, 0:HT], op0=ALU.mult, op1=ALU.add),
                             reads=[zb, yb, ppb], writes=[yb])
                        S.op("dve", lambda e, z=z, y=y, w0=w0: e.scalar_tensor_tensor(out=y[:, 0:HT], in0=z[:, 0:HT], scalar=w0, in1=y[:, 0:HT], op0=ALU.mult, op1=ALU.add),
                             reads=[zb, yb, ppb], writes=[yb])
                        res[("y", hf)] = (y, yb)
                    elif blk == 2:
                        y, yb = res[("y", hf)]
                        S.op("dve", lambda e, y=y, ps=ps: e.tensor_tensor(out=y[:, 0:HT], in0=ps[:], in1=y[:, 0:HT], op=ALU.mult),
                             reads=[pb, yb], writes=[yb])
                    else:
                        y, yb = res[("y", hf)]
                        sa, sab = tmp()
                        S.op("act", lambda e, sa=sa, ps=ps: e.activation(out=sa[:, 0:HT], in_=ps[:], func=AF.Tanh, scale=0.5),
                             reads=[pb], writes=[sab])
                        S.op("dve", lambda e, y=y, sa=sa: e.scalar_tensor_tensor(out=y[:, 0:HT], in0=sa[:, 0:HT], scalar=1.0, in1=y[:, 0:HT], op0=ALU.add, op1=ALU.mult),
                             reads=[sab, yb], writes=[yb])
            for hf in range(NH):
                ps, pb = proj_unit(gl, 4, hf, False)
                gu, gub = yg_ap(hf, 1), ygb[hf][1]
                gelu_chain(ps, pb, gu[:, 0:HT], gub)
                res[("gu", hf)] = (gu, gub)
            for hf in range(NH):
                ps, pb = bank()
                for tc in range(NTC):
                    vn = R_ap(hf, 2 * tc + (j // 4))[:, (j % 4) * 128:(j % 4) * 128 + 128]
                    rb_ = Rb[hf][2 * tc + (j // 4)]
                    S.op("pe", mm(ps[:, tc * 128:(tc + 1) * 128], vn, wmt[:, pi * 1024 + j * 128:pi * 1024 + (j + 1) * 128], True, True),
                         reads=[rb_, wmtbs[pi]], writes=[pb])
                gu, gub = res[("gu", hf)]
                t2, t2b = tmp()
                lgc = pp[:, 168 + l * 8 + j:168 + l * 8 + j + 1]
                ej = Et[:, pi * 1024 + j * 128:pi * 1024 + (j + 1) * 128].rearrange("p (o t) -> p o t", o=1).broadcast_to([128, NTC, 128])
                S.op("dve", lambda e, t2=t2, ps=ps, lgc=lgc, ej=ej: e.scalar_tensor_tensor(
                    out=t2[:, 0:HT].rearrange("p (c t) -> p c t", c=NTC), in0=ps[:].rearrange("p (c t) -> p c t", c=NTC),
                    scalar=lgc, in1=ej, op0=ALU.mult, op1=ALU.add),
                    reads=[pb, Ebs[pi], ppb], writes=[t2b])
                S.op("dve", lambda e, gu=gu, t2=t2: e.tensor_tensor(out=gu[:, 0:HT], in0=t2[:, 0:HT], in1=gu[:, 0:HT], op=ALU.mult),
                     reads=[t2b, gub], writes=[gub])
            for hf in range(NH):
                ps, pb = proj_unit(gl, 5, hf, hf == NH - 1)
                gu, gub = res[("gu", hf)]
                y, yb = res[("y", hf)]
                sbt, sbb = tmp()
                S.op("act", lambda e, sbt=sbt, ps=ps: e.activation(out=sbt[:, 0:HT], in_=ps[:], func=AF.Tanh, scale=0.5),
                     reads=[pb], writes=[sbb])
                S.op("dve", lambda e, sbt=sbt, gu=gu: e.scalar_tensor_tensor(out=gu[:, 0:HT], in0=sbt[:, 0:HT], scalar=1.0, in1=gu[:, 0:HT], op0=ALU.add, op1=ALU.mult),
                     reads=[sbb, gub], writes=[gub])
                S.op("dve", lambda e, gu=gu, y=y, hf=hf: e.tensor_tensor(out=R_ap(hf, 8 + j), in0=gu[:, 0:HT], in1=y[:, 0:HT], op=ALU.add),
                     reads=[gub, yb], writes=[Rb[hf][8 + j]])

        def wout_unit(gbase, xbi, e, hf, last_user):
            gl = gbase + 10 + e // 4
            ps, pb = bank()
            w = wblk(gl, e % 4)
            for k in range(8):
                o = S.op("pe", mm(ps[:], w[:, k * 128:(k + 1) * 128], R_ap(hf, 8 + k), k == 0, k == 7),
                         reads=[Rb[hf][8 + k]], writes=[pb], extra=[load_op[gl]])
            if last_user and e % 4 == 3:
                release(gl, o)
            S.op("dve", lambda e_, ps=ps, e=e, hf=hf: e_.tensor_tensor(out=x_ap(xbi, hf, e), in0=ps[:], in1=x_ap(xbi, hf, e), op=ALU.add),
                 reads=[pb, xb[xbi][hf][e]], writes=[xb[xbi][hf][e]])
            square_chunk(xbi, hf, e)

        def ff1_group(gbase, g, hf, last_user):
            gl = gbase + 12 + g
            for f in range(4 * g, 4 * g + 4):
                ps, pb = proj_unit(gl, f - 4 * g, hf, last_user and f == 4 * g + 3)
                t, tb = tmp()
                S.op("act", lambda e, t=t, ps=ps: e.activation(out=t[:, 0:HT], in_=ps[:], func=AF.Relu),
                     reads=[pb], writes=[tb])
                S.op("dve", lambda e, t=t, f=f, hf=hf: e.tensor_tensor(out=R_ap(hf, f), in0=t[:, 0:HT], in1=t[:, 0:HT], op=ALU.mult),
                     reads=[tb], writes=[Rb[hf][f]])

        def ff2_unit(gbase, xbi, e, hf, last_user):
            gl = gbase + 20 + e
            ps, pb = bank()
            for f in range(32):
                w = wblk(gl, f // 8)[:, (f % 8) * 128:(f % 8) * 128 + 128]
                o = S.op("pe", mm(ps[:], w, R_ap(hf, f), f == 0, f == 31),
                         reads=[Rb[hf][f]], writes=[pb], extra=[load_op[gl]])
            if last_user:
                release(gl, o)
            S.op("dve", lambda e_, ps=ps, e=e, hf=hf: e_.tensor_tensor(out=x_ap(xbi, hf, e), in0=ps[:], in1=x_ap(xbi, hf, e), op=ALU.add),
                 reads=[pb, xb[xbi][hf][e]], writes=[xb[xbi][hf][e]])
            square_chunk(xbi, hf, e)

        def load_x_half(tile, hf, to_R=False):
            xbi = tile % NXB
            src = xT[tile * NH + hf]
            if to_R:
                dst = RR[:, (hf * 32 + 16) * HT:(hf * 32 + 32) * HT].bitcast(F32)
                wr = [Rb[hf][i] for i in range(16, 32)]
            else:
                o = (xbi * NH + hf) * 8 * HT
                dst = xs[:, o:o + 8 * HT]
                wr = [xb[xbi][hf][c] for c in range(8)]
            S.op("sp", lambda e, dst=dst, src=src: e.dma_start(out=dst, in_=src), writes=wr, dma=True)

        assert NH == 2 and NXB == 1
        A, B = 0, 1
        all_stores = []
        load_x_half(0, A)
        load_x_half(0, B)
        xbi = 0
        layer_params_load(0, 0)
        issue_loads()
        norm_post(xbi, A, norm_pre(xbi, A, True), 0)
        nlayer = 0
        for tile in range(NT):
            for l in range(DEPTH):
                gbase = (tile * DEPTH + l) * NLD
                pi = nlayer % 2
                nlayer += 1
                has_next = nlayer < NT * DEPTH
                nl = (l + 1) % DEPTH
                v_units(gbase, A, 0)
                bR = tile > 0 and l == 0
                norm_post(xbi, B, norm_pre(xbi, B, bR or nlayer == 1, fromR=bR), l * 8, fromR=bR)
                if bR:
                    copy_x_from_R(xbi, A)
                    copy_x_from_R(xbi, B)
                v_units(gbase, A, 1)
                v_units(gbase, A, 2)
                v_units(gbase, A, 3)
                v_finish(A)
                for tc in range(NTC):
                    v_units(gbase, B, tc)
                v_finish(B)
                if nlayer == 1:
                    layer_params_compute(0, 0)
                for j in range(8):
                    mix_j(l, gbase + 2 + j, j, xbi, pi)
                act_preload(AF.Sqrt)
                for e in range(8):
                    wout_unit(gbase, xbi, e, A, False)
                for e in range(2):
                    wout_unit(gbase, xbi, e, B, True)
                norm_post(xbi, A, norm_pre(xbi, A), 32 + l * 8)
                for e in range(2, 8):
                    wout_unit(gbase, xbi, e, B, True)
                if has_next:
                    layer_params_load(nl, 1 - pi)
                ff1_group(gbase, 0, A, False)
                norm_post(xbi, B, norm_pre(xbi, B), 32 + l * 8)
                ff1_group(gbase, 1, A, False)
                ff1_group(gbase, 0, B, True)
                ff1_group(gbase, 1, B, True)
                for g in range(2, 8):
                    ff1_group(gbase, g, A, False)
                    ff1_group(gbase, g, B, True)
                act_preload(AF.Sqrt)
                if has_next:
                    layer_params_compute(nl, 1 - pi)
                for e in range(8):
                    ff2_unit(gbase, xbi, e, A, False)
                    if e > 0:
                        ff2_unit(gbase, xbi, e - 1, B, True)
                last = (l == DEPTH - 1)
                if not last:
                    norm_post(xbi, A, norm_pre(xbi, A), (l + 1) * 8)
                    ff2_unit(gbase, xbi, 7, B, True)
                else:
                    more = tile + 1 < NT
                    if more:
                        load_x_half(tile + 1, A, to_R=True)
                    all_stores += norm_post(xbi, A, norm_pre(xbi, A), 64, final=True, tile=tile)
                    ff2_unit(gbase, xbi, 7, B, True)
                    if more:
                        load_x_half(tile + 1, B, to_R=True)
                        norm_post(xbi, A, norm_pre(xbi, A, True, fromR=True), 0, fromR=True)
                    all_stores += norm_post(xbi, B, norm_pre(xbi, B), 64, final=True, tile=tile)
        assert wstate["next"] == tot_loads

        for e in ("pe", "act", "dve", "pool"):
            n = 0
            for o in S.q[e]:
                if o.is_dma:
                    continue
                if o.needed:
                    n += 1
                    o.ms = n
        for q in ("pool", "sp"):
            for i, o in enumerate(S.dma_ops[q]):
                o.dsem = dsems[q][i % 8]
                o.dval = 16 * (i // 8 + 1)

        def emitter(engname, final_ops=()):
            def body(eng):
                seen = {}

                def wait_for(d):
                    if d.is_dma:
                        sem, val = d.dsem, d.dval
                    else:
                        sem, val = prog[d.eng], d.ms
                    if seen.get(id(sem), 0) >= val:
                        return
                    eng.wait_ge(sem, val)
                    seen[id(sem)] = val
                for o in S.q[engname]:
                    for d in o.deps:
                        wait_for(d)
                    ins = o.fn(eng)
                    if o.is_dma:
                        ins.then_inc(o.dsem, 16)
                    elif o.needed:
                        ins.then_inc(prog[o.eng], 1)
                for d in final_ops:
                    wait_for(d)
            return body

        with nc.Block() as block:
            block.tensor(emitter("pe"))
            block.scalar(emitter("act"))
            block.vector(emitter("dve"))
            block.gpsimd(emitter("pool"))
            block.sync(emitter("sp", all_stores))
    return nc


def _prep_weights(w_in, w_out, w_ff1, w_ff2, depth):
    wst = np.empty((depth, 128, 128, 1024), np.float32)
    for l in range(depth):
        wi = w_in[l].reshape(8, 128, 7, 1024)
        wv = wi[:, :, 4, :].reshape(8, 128, 2, 512)
        wv = wv.transpose(1, 2, 0, 3).reshape(128, 2, 4, 1024)
        wst[l, :, 0:8, :] = wv.reshape(128, 8, 1024)
        gsel = [2, 0, 1, 5, 3, 6]
        wj = wi[:, :, gsel, :].reshape(8, 128, 6, 8, 128)
        wj = wj.transpose(1, 3, 2, 0, 4)
        wst[l, :, 8:56, :] = wj.reshape(128, 48, 1024)
        wo = w_out[l].reshape(8, 128, 8, 128).transpose(1, 2, 0, 3)
        wst[l, :, 56:64, :] = wo.reshape(128, 8, 1024)
        w1 = w_ff1[l].reshape(8, 128, 32, 128).transpose(1, 2, 0, 3)
        wst[l, :, 64:96, :] = w1.reshape(128, 32, 1024)
        w2 = w_ff2[l].reshape(32, 128, 8, 128).transpose(1, 2, 0, 3)
        wst[l, :, 96:128, :] = w2.reshape(128, 32, 1024)
    return wst.reshape(depth, 128, 128 * 1024)


def _prep_small(norm_mix, conv_w, sgu_w, sgu_b, sgu_ln_g, sgu_ln_b, norm_mlp, final_norm, depth):
    pp = np.zeros((128, NPP), np.float32)
    for l in range(depth):
        pp[:, l * 8:(l + 1) * 8] = norm_mix[l].reshape(8, 128).T
        pp[:, 32 + l * 8:32 + (l + 1) * 8] = norm_mlp[l].reshape(8, 128).T
        for k in range(3):
            pp[:, 72 + l * 24 + k * 8:72 + l * 24 + (k + 1) * 8] = conv_w[l, k].reshape(8, 128).T
    pp[:, 64:72] = final_norm.reshape(8, 128).T
    for l in range(depth):
        pp[:, 168 + l * 8:168 + (l + 1) * 8] = sgu_ln_g[l].reshape(8, 128).T
        pp[:, 200 + l * 8:200 + (l + 1) * 8] = sgu_ln_b[l].reshape(8, 128).T
    sgw = np.ascontiguousarray(sgu_w[:depth].transpose(0, 3, 1, 2)).reshape(depth, 128, 1024)
    sgb = np.ascontiguousarray(np.broadcast_to(sgu_b[:depth].reshape(depth, 1, 1024), (depth, 128, 1024)))
    return pp, sgw, sgb


_CFG = dict(DEPTH=4, NT=4, NH=2, RB=20, NXB=1, NTMP=6)
_NC_CACHE = {}


def _run(x, norm_mix, w_in, conv_w, sgu_w, sgu_b, sgu_ln_g, sgu_ln_b, w_out, norm_mlp, w_ff1, w_ff2,
         final_norm, cfg, seq=SEQ, trace=False):
    depth = cfg["DEPTH"]
    key = tuple(sorted(cfg.items()))
    if key not in _NC_CACHE:
        _NC_CACHE[key] = _build(**cfg)
    nc = _NC_CACHE[key]
    f = lambda a: np.asarray(a, dtype=np.float32)
    wst = _prep_weights(f(w_in), f(w_out), f(w_ff1), f(w_ff2), depth)
    pp, sgw, sgb = _prep_small(f(norm_mix), f(conv_w), f(sgu_w), f(sgu_b), f(sgu_ln_g), f(sgu_ln_b),
                                    f(norm_mlp), f(final_norm), depth)
    x = f(x)
    in_maps = []
    for b in range(NCORES):
        xt = x[b, :seq].reshape(seq // HT, HT, 8, 128).transpose(0, 3, 2, 1)
        in_maps.append({"xT": np.ascontiguousarray(xt).reshape(seq // HT, 128, 8 * HT), "wst": wst, "pp": pp,
                        "sgw": sgw, "sgb": sgb})
    res = run_bass_kernel_spmd(nc, in_maps, core_ids=list(range(NCORES)), trace=trace)
    out = np.stack([np.ascontiguousarray(
        res.results[b]["outT"].reshape(seq // HT, 128, 8, HT).transpose(0, 3, 2, 1)).reshape(seq, D)
        for b in range(NCORES)], axis=0)
    return out.astype(np.float32), res


def kernel(x, norm_mix, w_in, conv_w, sgu_w, sgu_b, sgu_ln_g, sgu_ln_b, w_out, norm_mlp, w_ff1, w_ff2,
           final_norm):
    out, _ = _run(x, norm_mix, w_in, conv_w, sgu_w, sgu_b, sgu_ln_g, sgu_ln_b, w_out, norm_mlp, w_ff1,
                  w_ff2, final_norm, _CFG)
    return out
```

```python
import numpy as np
from contextlib import ExitStack
import concourse.bass as bass
import concourse.mybir as mybir
from concourse.bass_utils import run_bass_kernel_spmd

F32 = mybir.dt.float32
BF16 = mybir.dt.bfloat16
AF = mybir.ActivationFunctionType
ALU = mybir.AluOpType

D = 1024
SEQ = 4096
NCORES = 8
EPS = 1e-6
C1 = 1.5957691216057308
C2S = 0.21145921592173913
HT = 512
LOADS = ([(0, 4), (4, 4)] + [(8 + 6 * j, 6) for j in range(8)] + [(56, 4), (60, 4)]
         + [(64 + 4 * i, 4) for i in range(8)] + [(96 + 4 * e, 4) for e in range(8)])
NLD = len(LOADS)
NPP = 232


class Op:
    __slots__ = ("eng", "fn", "deps", "ms", "is_dma", "dsem", "dval", "needed")

    def __init__(self, eng, fn, deps, is_dma):
        self.eng = eng
        self.fn = fn
        self.deps = deps
        self.is_dma = is_dma
        self.ms = 0
        self.dsem = None
        self.dval = 0
        self.needed = False


class Buf:
    __slots__ = ("w", "r")

    def __init__(self):
        self.w = None
        self.r = {}


class Sched:
    ENGS = ("pe", "act", "dve", "pool", "sp")

    def __init__(self):
        self.q = {e: [] for e in self.ENGS}
        self.ndma = {"pool": 0, "sp": 0}
        self.dma_ops = {"pool": [], "sp": []}

    def op(self, eng, fn, reads=(), writes=(), extra=(), dma=False):
        deps = []

        def add(d, raw):
            if d is None:
                return
            if d.is_dma or dma:
                deps.append(d)
            elif d.eng != eng:
                deps.append(d)
            elif eng != "pe":
                deps.append(d)
        for b in reads:
            add(b.w, True)
        for b in writes:
            add(b.w, False)
            for r in b.r.values():
                add(r, False)
        for d in extra:
            if d is not None:
                deps.append(d)
        o = Op(eng, fn, deps, dma)
        if dma:
            lst = self.dma_ops[eng]
            if len(lst) >= 8:
                o.deps.append(lst[-8])
            lst.append(o)
        for d in o.deps:
            d.needed = True
        key = ("dma", id(o)) if dma else eng
        for b in reads:
            b.r[key] = o
        for b in writes:
            b.w = o
            b.r = {}
        self.q[eng].append(o)
        return o


def _build(DEPTH, NT, NH, RB, NXB, NTMP=6):
    T = NH * HT
    NTC = HT // 128
    nc = bass.Bass("TRN2", target_bir_lowering=False)
    xT = nc.dram_tensor("xT", [NT * NH, 128, 8 * HT], F32, kind="ExternalInput").ap()
    wst = nc.dram_tensor("wst", [DEPTH, 128, 128 * 1024], F32, kind="ExternalInput").ap()
    ppd = nc.dram_tensor("pp", [128, NPP], F32, kind="ExternalInput").ap()
    sgwd = nc.dram_tensor("sgw", [DEPTH, 128, 1024], F32, kind="ExternalInput").ap()
    sgbd = nc.dram_tensor("sgb", [DEPTH, 128, 1024], F32, kind="ExternalInput").ap()
    outT = nc.dram_tensor("outT", [NT * NH, 128, 8 * HT], F32, kind="ExternalOutput").ap()

    S = Sched()
    with ExitStack() as es:
        def sb(name, shape, dt):
            return es.enter_context(nc.sbuf_tensor(name, shape, dt))
        xs = sb("xs", [128, NXB * 8 * T], F32)
        hh = sb("hh", [128, NH * 8 * HT], BF16)
        RR = sb("RR", [128, NH * 32 * HT], BF16)
        tmpt = sb("tmpt", [128, NTMP * 516], F32)
        ygt = sb("ygt", [128, NH * 2 * HT], F32)
        gvt = sb("gvt", [128, 4 * 1024], F32)
        ring = sb("ring", [128, RB * 1024], BF16)
        ones_bf = sb("ones_bf", [128, 128], BF16)
        ones1 = sb("ones1", [128, 128], BF16)
        pp = sb("pps", [128, NPP], F32)
        Et = sb("Et", [128, 2 * 1024], F32)
        bsbc = sb("bsbc", [128, 1024], F32)
        wmt = sb("wmt", [128, 2 * 1024], BF16)
        carry = sb("carry", [128, DEPTH * 8 * 2], F32)
        scr = sb("scr", [128, 4], F32)
        stt = sb("stt", [128, 4 * 12], F32)
        mvt = sb("mvt", [128, 4 * 4], F32)
        psums = [es.enter_context(nc.psum_tensor(f"ps{i}", [128, 512], F32)) for i in range(8)]
        psb = [Buf() for _ in range(8)]
        prog = {e: es.enter_context(nc.semaphore("sem_" + e)) for e in ("pe", "act", "dve", "pool")}
        dsems = {q: [es.enter_context(nc.semaphore(f"dma_{q}{i}")) for i in range(8)] for q in ("pool", "sp")}

        xb = [[[Buf() for _ in range(8)] for _ in range(NH)] for _ in range(NXB)]
        hb = [[Buf() for _ in range(8)] for _ in range(NH)]
        Rb = [[Buf() for _ in range(32)] for _ in range(NH)]
        tmpb = [Buf() for _ in range(NTMP)]
        ygb = [[Buf() for _ in range(2)] for _ in range(NH)]
        gvb = [Buf() for _ in range(4)]
        stb = [Buf() for _ in range(4)]
        onesb, ppb, bsbcb = Buf(), Buf(), Buf()
        Ebs, wmtbs = [Buf(), Buf()], [Buf(), Buf()]
        carryb = [[Buf() for _ in range(8)] for _ in range(DEPTH)]

        def x_ap(xbi, hf, c):
            o = ((xbi * NH + hf) * 8 + c) * HT
            return xs[:, o:o + HT]

        def h_ap(hf, k, lo=0, n=HT):
            o = (hf * 8 + k) * HT + lo
            return hh[:, o:o + n]

        def R_ap(hf, slot, n=HT):
            o = (hf * 32 + slot) * HT
            return RR[:, o:o + n]

        def yg_ap(hf, i):
            o = (hf * 2 + i) * HT
            return ygt[:, o:o + HT]

        cnt = {"tmp": 0, "bank": 0, "gv": 0}

        def tmp():
            i = cnt["tmp"] % NTMP
            cnt["tmp"] += 1
            return tmpt[:, i * 516:(i + 1) * 516], tmpb[i]

        def bank():
            i = cnt["bank"] % 8
            cnt["bank"] += 1
            return psums[i], psb[i]

        tot_loads = NT * DEPTH * NLD
        place = []
        free_dep = []
        live = []
        pos = 0
        for gl in range(tot_loads):
            nb = LOADS[gl % NLD][1]
            if pos + nb > RB:
                pos = 0
            st, en = pos, pos + nb
            dep = -1
            keep = []
            for (g2, s2, e2) in live:
                if s2 < en and st < e2:
                    dep = max(dep, g2)
                else:
                    keep.append((g2, s2, e2))
            live = keep + [(gl, st, en)]
            place.append(st)
            free_dep.append(dep)
            pos = en
        wstate = {"next": 0}
        load_op = [None] * tot_loads
        rel_op = [None] * tot_loads

        def issue_loads():
            while wstate["next"] < tot_loads:
                gl = wstate["next"]
                fd = free_dep[gl]
                if fd >= 0 and rel_op[fd] is None:
                    return
                l = (gl // NLD) % DEPTH
                b0, nb = LOADS[gl % NLD]
                dst = ring[:, place[gl] * 1024:(place[gl] + nb) * 1024]
                src = wst[l, :, b0 * 1024:(b0 + nb) * 1024]
                load_op[gl] = S.op("pool", lambda e, dst=dst, src=src: e.dma_start(out=dst, in_=src),
                                   extra=[rel_op[fd]] if fd >= 0 else [], dma=True)
                wstate["next"] += 1

        def wblk(gl, b):
            assert load_op[gl] is not None, "weight ring too small"
            o = (place[gl] + b) * 1024
            return ring[:, o:o + 1024]

        def release(gl, op):
            rel_op[gl] = op
            issue_loads()

        def mm(out, lhsT, rhs, start, stop):
            return lambda e: e.matmul(out, lhsT=lhsT, rhs=rhs, start=start, stop=stop)

        S.op("dve", lambda e: e.memset(ones_bf[:], 1.0 / 1024.0), writes=[onesb])
        S.op("dve", lambda e: e.memset(ones1[:], 1.0), writes=[onesb])
        scrb, scrob = Buf(), Buf()
        S.op("dve", lambda e: e.memset(scr[:], 1.0), writes=[scrb])

        def act_preload(func):
            S.op("act", lambda e: e.activation(out=scr[:, 1:2], in_=scr[:, 0:1], func=func), reads=[scrb], writes=[scrob])

        S.op("dve", lambda e: e.memset(carry[:], 0.0), writes=[b for l in carryb for b in l])
        S.op("sp", lambda e: e.dma_start(out=pp[:], in_=ppd), writes=[ppb], dma=True)
        S.op("dve", lambda e: e.tensor_scalar_mul(out=pp[:, 72:200], in0=pp[:, 72:200], scalar1=0.5),
             reads=[ppb], writes=[ppb])

        def load_x(tile):
            xbi = tile % NXB
            for hf in range(NH):
                for c in range(8):
                    dst = x_ap(xbi, hf, c)
                    src = xT[c * 128:(c + 1) * 128, tile * T + hf * HT:tile * T + (hf + 1) * HT]
                    S.op("sp", lambda e, dst=dst, src=src: e.dma_start(out=dst, in_=src),
                         writes=[xb[xbi][hf][c]], dma=True)

        def layer_params_load(l, pi):
            wm = wmt[:, pi * 1024:(pi + 1) * 1024]
            S.op("sp", lambda e: e.dma_start(out=bsbc[:], in_=sgbd[l]), writes=[bsbcb], dma=True)
            S.op("pool", lambda e: e.dma_start(out=wm, in_=sgwd[l]), writes=[wmtbs[pi]], dma=True)
            S.op("pool", lambda e: e.affine_select(out=wm, in_=wm, pattern=[[0, 8], [1, 128]],
                                                     compare_op=ALU.is_ge, fill=0.0, base=0,
                                                     channel_multiplier=-1),
                 reads=[wmtbs[pi]], writes=[wmtbs[pi]])

        def layer_params_compute(l, pi):
            wm = wmt[:, pi * 1024:(pi + 1) * 1024]
            Ep = Et[:, pi * 1024:(pi + 1) * 1024]
            for hlf in range(2):
                ps, pb = bank()
                S.op("pe", mm(ps[:], ones1[:], wm[:, hlf * 512:(hlf + 1) * 512], True, True),
                     reads=[wmtbs[pi], onesb], writes=[pb])
                for jj in range(4):
                    j = hlf * 4 + jj
                    lbc = pp[:, 200 + l * 8 + j:200 + l * 8 + j + 1]
                    S.op("dve", lambda e, ps=ps, jj=jj, j=j, lbc=lbc: e.scalar_tensor_tensor(
                        out=Ep[:, j * 128:(j + 1) * 128], in0=ps[:, jj * 128:(jj + 1) * 128], scalar=lbc,
                        in1=bsbc[:, j * 128:(j + 1) * 128], op0=ALU.mult, op1=ALU.add),
                        reads=[pb, ppb, bsbcb], writes=[Ebs[pi]])
            S.op("dve", lambda e: e.tensor_scalar_mul(out=Ep, in0=Ep, scalar1=0.5), reads=[Ebs[pi]], writes=[Ebs[pi]])

        def xr_ap(hf, c):
            o = (hf * 32 + 16 + 2 * c) * HT
            return RR[:, o:o + 2 * HT].bitcast(F32)

        def xsrc(xbi, hf, c, fromR):
            if fromR:
                return xr_ap(hf, c), [Rb[hf][16 + 2 * c], Rb[hf][17 + 2 * c]]
            return x_ap(xbi, hf, c), [xb[xbi][hf][c]]

        def square_chunk(xbi, hf, c, fromR=False):
            src, sbufs = xsrc(xbi, hf, c, fromR)
            S.op("act", lambda e: e.activation(out=h_ap(hf, c), in_=src, func=AF.Square),
                 reads=sbufs, writes=[hb[hf][c]])

        def copy_x_from_R(xbi, hf):
            for c in range(8):
                src, sbufs = xsrc(xbi, hf, c, True)
                S.op("pool", lambda e, c=c, src=src: e.tensor_copy(out=x_ap(xbi, hf, c), in_=src),
                     reads=sbufs, writes=[xb[xbi][hf][c]])

        def norm_pre(xbi, hf, squares=False, fromR=False):
            if squares:
                for c in range(8):
                    square_chunk(xbi, hf, c, fromR)
            ps, pb = bank()
            for c in range(8):
                S.op("pe", mm(ps[:], ones_bf[:], h_ap(hf, c), c == 0, c == 7),
                     reads=[hb[hf][c], onesb], writes=[pb])
            return ps, pb

        def norm_post(xbi, hf, pre, gcol, final=False, tile=0, fromR=False):
            ps, pb = pre
            t, tb = tmp()
            S.op("act", lambda e: e.activation(out=t[:, 0:HT], in_=ps[:], func=AF.Sqrt, bias=EPS),
                 reads=[pb], writes=[tb])
            if not final and gcol < 32:
                act_preload(AF.Gelu_apprx_tanh)
            S.op("dve", lambda e: e.reciprocal(out=t[:, 0:HT], in_=t[:, 0:HT]), reads=[tb], writes=[tb])
            stores = []
            for c in range(8):
                g = pp[:, gcol + c:gcol + c + 1]
                if not final:
                    src, sbufs = xsrc(xbi, hf, c, fromR)
                    S.op("dve", lambda e, c=c, g=g, src=src: e.scalar_tensor_tensor(
                        out=h_ap(hf, c), in0=src, scalar=g, in1=t[:, 0:HT],
                        op0=ALU.mult, op1=ALU.mult),
                        reads=sbufs + [tb, ppb], writes=[hb[hf][c]])
                else:
                    o32 = RR[:, (hf * 32 + 2 * c) * HT:(hf * 32 + 2 * c + 2) * HT].bitcast(F32)
                    S.op("dve", lambda e, c=c, g=g, o32=o32: e.scalar_tensor_tensor(
                        out=o32, in0=x_ap(xbi, hf, c), scalar=g, in1=t[:, 0:HT],
                        op0=ALU.mult, op1=ALU.mult),
                        reads=[xb[xbi][hf][c], tb, ppb], writes=[Rb[hf][2 * c], Rb[hf][2 * c + 1]])
            if final:
                src = RR[:, hf * 32 * HT:hf * 32 * HT + 16 * HT].bitcast(F32)
                dst = outT[tile * NH + hf]
                stores.append(S.op("sp", lambda e, src=src, dst=dst: e.dma_start(out=dst, in_=src),
                                   reads=[Rb[hf][i] for i in range(16)], dma=True))
            return stores

        def gelu_chain(ps, pb, out_ap, outb, extra_reads=()):
            S.op("act", lambda e: e.activation(out=out_ap, in_=ps[:], func=AF.Gelu_apprx_tanh),
                 reads=[pb] + list(extra_reads), writes=[outb])

        def v_units(gbase, hf, tc):
            gv0, gv1 = gbase + 0, gbase + 1
            gv = gvt[:, tc * 1024:(tc + 1) * 1024]
            gb = gvb[tc]
            for ch in range(2):
                gl = (gv0, gv1)[ch]
                ps, pb = bank()
                for k in range(8):
                    w = wblk(gl, k // 2)[:, (k % 2) * 512:(k % 2) * 512 + 512]
                    o = S.op("pe", mm(ps[:], h_ap(hf, k, tc * 128, 128), w, k == 0, k == 7),
                             reads=[hb[hf][k]], writes=[pb], extra=[load_op[gl]])
                if hf == NH - 1 and tc == NTC - 1:
                    release(gl, o)
                gelu_chain(ps, pb, gv[:, ch * 512:(ch + 1) * 512], gb)
            st = stt[:, tc * 12:(tc + 1) * 12]
            mv = mvt[:, tc * 4:(tc + 1) * 4]
            sb_ = stb[tc]
            S.op("dve", lambda e, st=st, gv=gv: e.bn_stats(out=st[:, 0:6], in_=gv[:, 0:512]),
                 reads=[gb], writes=[sb_])
            S.op("dve", lambda e, st=st, gv=gv: e.bn_stats(out=st[:, 6:12], in_=gv[:, 512:1024]),
                 reads=[gb], writes=[sb_])
            S.op("dve", lambda e, st=st, mv=mv: e.bn_aggr(out=mv[:, 0:2], in_=st[:, 0:12]),
                 reads=[sb_], writes=[sb_])

        def v_finish(hf):
            for tc in range(NTC):
                mv = mvt[:, tc * 4:(tc + 1) * 4]
                S.op("act", lambda e, mv=mv: e.activation(out=mv[:, 2:3], in_=mv[:, 1:2], func=AF.Sqrt, bias=EPS),
                     reads=[stb[tc]], writes=[stb[tc]])
            act_preload(AF.Gelu_apprx_tanh)
            for tc in range(NTC):
                mv = mvt[:, tc * 4:(tc + 1) * 4]
                S.op("dve", lambda e, mv=mv: e.reciprocal(out=mv[:, 2:3], in_=mv[:, 2:3]), reads=[stb[tc]], writes=[stb[tc]])
                S.op("dve", lambda e, mv=mv: e.scalar_tensor_tensor(out=mv[:, 3:4], in0=mv[:, 0:1], scalar=-1.0,
                                                                   in1=mv[:, 2:3], op0=ALU.mult, op1=ALU.mult),
                     reads=[stb[tc]], writes=[stb[tc]])
            for tc in range(NTC):
                mv = mvt[:, tc * 4:(tc + 1) * 4]
                gv = gvt[:, tc * 1024:(tc + 1) * 1024]
                vn = RR[:, (hf * 32 + 2 * tc) * HT:(hf * 32 + 2 * tc + 2) * HT]
                if tc % 2 == 0:
                    S.op("act", lambda e, mv=mv, gv=gv, vn=vn: e.activation(out=vn, in_=gv[:, :], func=AF.Identity,
                                                                            scale=mv[:, 2:3], bias=mv[:, 3:4]),
                         reads=[gvb[tc], stb[tc]], writes=[Rb[hf][2 * tc], Rb[hf][2 * tc + 1]])
                else:
                    S.op("dve", lambda e, mv=mv, gv=gv, vn=vn: e.tensor_scalar(out=vn, in0=gv[:, :], scalar1=mv[:, 0:1],
                                                                             scalar2=mv[:, 2:3], op0=ALU.subtract,
                                                                             op1=ALU.mult),
                         reads=[gvb[tc], stb[tc]], writes=[Rb[hf][2 * tc], Rb[hf][2 * tc + 1]])

        def proj_unit(gl, blk, hf, last):
            ps, pb = bank()
            w = wblk(gl, blk)
            for k in range(8):
                o = S.op("pe", mm(ps[:], w[:, k * 128:(k + 1) * 128], h_ap(hf, k), k == 0, k == 7),
                         reads=[hb[hf][k]], writes=[pb], extra=[load_op[gl]])
            if last:
                release(gl, o)
            return ps, pb

        def mix_j(l, gl, j, xbi, pi):
            cw = 72 + l * 24
            res = {}
            for blk in range(4):
                for hf in range(NH):
                    res[(blk, hf)] = proj_unit(gl, blk, hf, False)
                for hf in range(NH):
                    ps, pb = res[(blk, hf)]
                    if blk == 0:
                        ta, tab = tmp()
                        S.op("act", lambda e, ta=ta, ps=ps: e.activation(out=ta[:, 0:HT], in_=ps[:], func=AF.Copy),
                             reads=[pb], writes=[tab])
                        res[("a", hf)] = (ta, tab)
                    elif blk == 1:
                        ta, tab = res[("a", hf)]
                        z, zb = tmp()
                        y, yb = yg_ap(hf, 0), ygb[hf][0]
                        cb = carryb[l][j]
                        cap = carry[:, (l * 8 + j) * 2:(l * 8 + j) * 2 + 2]
                        S.op("dve", lambda e, z=z, cap=cap: e.tensor_copy(out=z[:, 0:2], in_=cap), reads=[cb], writes=[zb])
                        S.op("dve", lambda e, z=z, ps=ps, ta=ta: e.tensor_tensor(out=z[:, 2:2 + HT], in0=ps[:], in1=ta[:, 0:HT], op=ALU.mult),
                             reads=[pb, tab], writes=[zb])
                        S.op("dve", lambda e, z=z, cap=cap: e.tensor_copy(out=cap, in_=z[:, HT:HT + 2]), reads=[zb], writes=[cb])
                        w0 = pp[:, cw + 0 * 8 + j:cw + 0 * 8 + j + 1]
                        w1 = pp[:, cw + 1 * 8 + j:cw + 1 * 8 + j + 1]
                        w2 = pp[:, cw + 2 * 8 + j:cw + 2 * 8 + j + 1]
                        S.op("act", lambda e, z=z, y=y, w2=w2: e.activation(out=y[:, 0:HT], in_=z[:, 2:2 + HT], func=AF.Identity, scale=w2),
                             reads=[zb, ppb], writes=[yb])
                        S.op("dve", lambda e, z=z, y=y, w1=w1: e.scalar_tensor_tensor(out=y[:, 0:HT], in0=z[:, 1:1 + HT], scalar=w1, in1=y[:, 0:HT], op0=ALU.mult, op1=ALU.add),
                             reads=[zb, yb, ppb], writes=[yb])
                        S.op("dve", lambda e, z=z, y=y, w0=w0: e.scalar_tensor_tensor(out=y[:, 0:HT], in0=z[:, 0:HT], scalar=w0, in1=y[:, 0:HT], op0=ALU.mult, op1=ALU.add),
                             reads=[zb, yb, ppb], writes=[yb])
                        res[("y", hf)] = (y, yb)
                    elif blk == 2:
                        y, yb = res[("y", hf)]
                        S.op("dve", lambda e, y=y, ps=ps: e.tensor_tensor(out=y[:, 0:HT], in0=ps[:], in1=y[:, 0:HT], op=ALU.mult),
                             reads=[pb, yb], writes=[yb])
                    else:
                        y, yb = res[("y", hf)]
                        sa, sab = tmp()
                        S.op("act", lambda e, sa=sa, ps=ps: e.activation(out=sa[:, 0:HT], in_=ps[:], func=AF.Tanh, scale=0.5),
                             reads=[pb], writes=[sab])
                        S.op("dve", lambda e, y=y, sa=sa: e.scalar_tensor_tensor(out=y[:, 0:HT], in0=sa[:, 0:HT], scalar=1.0, in1=y[:, 0:HT], op0=ALU.add, op1=ALU.mult),
                             reads=[sab, yb], writes=[yb])
            for hf in range(NH):
                ps, pb = proj_unit(gl, 4, hf, False)
                gu, gub = yg_ap(hf, 1), ygb[hf][1]
                gelu_chain(ps, pb, gu[:, 0:HT], gub)
                res[("gu", hf)] = (gu, gub)
            for hf in range(NH):
                ps, pb = bank()
                for tc in range(NTC):
                    vn = R_ap(hf, 2 * tc + (j // 4))[:, (j % 4) * 128:(j % 4) * 128 + 128]
                    rb_ = Rb[hf][2 * tc + (j // 4)]
                    S.op("pe", mm(ps[:, tc * 128:(tc + 1) * 128], vn, wmt[:, pi * 1024 + j * 128:pi * 1024 + (j + 1) * 128], True, True),
                         reads=[rb_, wmtbs[pi]], writes=[pb])
                gu, gub = res[("gu", hf)]
                t2, t2b = tmp()
                lgc = pp[:, 168 + l * 8 + j:168 + l * 8 + j + 1]
                ej = Et[:, pi * 1024 + j * 128:pi * 1024 + (j + 1) * 128].rearrange("p (o t) -> p o t", o=1).broadcast_to([128, NTC, 128])
                S.op("dve", lambda e, t2=t2, ps=ps, lgc=lgc, ej=ej: e.scalar_tensor_tensor(
                    out=t2[:, 0:HT].rearrange("p (c t) -> p c t", c=NTC), in0=ps[:].rearrange("p (c t) -> p c t", c=NTC),
                    scalar=lgc, in1=ej, op0=ALU.mult, op1=ALU.add),
                    reads=[pb, Ebs[pi], ppb], writes=[t2b])
                S.op("dve", lambda e, gu=gu, t2=t2: e.tensor_tensor(out=gu[:, 0:HT], in0=t2[:, 0:HT], in1=gu[:, 0:HT], op=ALU.mult),
                     reads=[t2b, gub], writes=[gub])
            for hf in range(NH):
                ps, pb = proj_unit(gl, 5, hf, hf == NH - 1)
                gu, gub = res[("gu", hf)]
                y, yb = res[("y", hf)]
                sbt, sbb = tmp()
                S.op("act", lambda e, sbt=sbt, ps=ps: e.activation(out=sbt[:, 0:HT], in_=ps[:], func=AF.Tanh, scale=0.5),
                     reads=[pb], writes=[sbb])
                S.op("dve", lambda e, sbt=sbt, gu=gu: e.scalar_tensor_tensor(out=gu[:, 0:HT], in0=sbt[:, 0:HT], scalar=1.0, in1=gu[:, 0:HT], op0=ALU.add, op1=ALU.mult),
                     reads=[sbb, gub], writes=[gub])
                S.op("dve", lambda e, gu=gu, y=y, hf=hf: e.tensor_tensor(out=R_ap(hf, 8 + j), in0=gu[:, 0:HT], in1=y[:, 0:HT], op=ALU.add),
                     reads=[gub, yb], writes=[Rb[hf][8 + j]])

        def wout_unit(gbase, xbi, e, hf, last_user):
            gl = gbase + 10 + e // 4
            ps, pb = bank()
            w = wblk(gl, e % 4)
            for k in range(8):
                o = S.op("pe", mm(ps[:], w[:, k * 128:(k + 1) * 128], R_ap(hf, 8 + k), k == 0, k == 7),
                         reads=[Rb[hf][8 + k]], writes=[pb], extra=[load_op[gl]])
            if last_user and e % 4 == 3:
                release(gl, o)
            S.op("dve", lambda e_, ps=ps, e=e, hf=hf: e_.tensor_tensor(out=x_ap(xbi, hf, e), in0=ps[:], in1=x_ap(xbi, hf, e), op=ALU.add),
                 reads=[pb, xb[xbi][hf][e]], writes=[xb[xbi][hf][e]])
            square_chunk(xbi, hf, e)

        def ff1_group(gbase, g, hf, last_user):
            gl = gbase + 12 + g
            for f in range(4 * g, 4 * g + 4):
                ps, pb = proj_unit(gl, f - 4 * g, hf, last_user and f == 4 * g + 3)
                t, tb = tmp()
                S.op("act", lambda e, t=t, ps=ps: e.activation(out=t[:, 0:HT], in_=ps[:], func=AF.Relu),
                     reads=[pb], writes=[tb])
                S.op("dve", lambda e, t=t, f=f, hf=hf: e.tensor_tensor(out=R_ap(hf, f), in0=t[:, 0:HT], in1=t[:, 0:HT], op=ALU.mult),
                     reads=[tb], writes=[Rb[hf][f]])

        def ff2_unit(gbase, xbi, e, hf, last_user):
            gl = gbase + 20 + e
            ps, pb = bank()
            for f in range(32):
                w = wblk(gl, f // 8)[:, (f % 8) * 128:(f % 8) * 128 + 128]
                o = S.op("pe", mm(ps[:], w, R_ap(hf, f), f == 0, f == 31),
                         reads=[Rb[hf][f]], writes=[pb], extra=[load_op[gl]])
            if last_user:
                release(gl, o)
            S.op("dve", lambda e_, ps=ps, e=e, hf=hf: e_.tensor_tensor(out=x_ap(xbi, hf, e), in0=ps[:], in1=x_ap(xbi, hf, e), op=ALU.add),
                 reads=[pb, xb[xbi][hf][e]], writes=[xb[xbi][hf][e]])
            square_chunk(xbi, hf, e)

        def load_x_half(tile, hf, to_R=False):
            xbi = tile % NXB
            src = xT[tile * NH + hf]
            if to_R:
                dst = RR[:, (hf * 32 + 16) * HT:(hf * 32 + 32) * HT].bitcast(F32)
                wr = [Rb[hf][i] for i in range(16, 32)]
            else:
                o = (xbi * NH + hf) * 8 * HT
                dst = xs[:, o:o + 8 * HT]
                wr = [xb[xbi][hf][c] for c in range(8)]
            S.op("sp", lambda e, dst=dst, src=src: e.dma_start(out=dst, in_=src), writes=wr, dma=True)

        assert NH == 2 and NXB == 1
        A, B = 0, 1
        all_stores = []
        load_x_half(0, A)
        load_x_half(0, B)
        xbi = 0
        layer_params_load(0, 0)
        issue_loads()
        norm_post(xbi, A, norm_pre(xbi, A, True), 0)
        nlayer = 0
        for tile in range(NT):
            for l in range(DEPTH):
                gbase = (tile * DEPTH + l) * NLD
                pi = nlayer % 2
                nlayer += 1
                has_next = nlayer < NT * DEPTH
                nl = (l + 1) % DEPTH
                v_units(gbase, A, 0)
                bR = tile > 0 and l == 0
                norm_post(xbi, B, norm_pre(xbi, B, bR or nlayer == 1, fromR=bR), l * 8, fromR=bR)
                if bR:
                    copy_x_from_R(xbi, A)
                    copy_x_from_R(xbi, B)
                v_units(gbase, A, 1)
                v_units(gbase, A, 2)
                v_units(gbase, A, 3)
                v_finish(A)
                for tc in range(NTC):
                    v_units(gbase, B, tc)
                v_finish(B)
                if nlayer == 1:
                    layer_params_compute(0, 0)
                for j in range(8):
                    mix_j(l, gbase + 2 + j, j, xbi, pi)
                act_preload(AF.Sqrt)
                for e in range(8):
                    wout_unit(gbase, xbi, e, A, False)
                for e in range(2):
                    wout_unit(gbase, xbi, e, B, True)
                norm_post(xbi, A, norm_pre(xbi, A), 32 + l * 8)
                for e in range(2, 8):
                    wout_unit(gbase, xbi, e, B, True)
                if has_next:
                    layer_params_load(nl, 1 - pi)
                ff1_group(gbase, 0, A, False)
                norm_post(xbi, B, norm_pre(xbi, B), 32 + l * 8)
                ff1_group(gbase, 1, A, False)
                ff1_group(gbase, 0, B, True)
                ff1_group(gbase, 1, B, True)
                for g in range(2, 8):
                    ff1_group(gbase, g, A, False)
                    ff1_group(gbase, g, B, True)
                act_preload(AF.Sqrt)
                if has_next:
                    layer_params_compute(nl, 1 - pi)
                for e in range(8):
                    ff2_unit(gbase, xbi, e, A, False)
                    if e > 0:
                        ff2_unit(gbase, xbi, e - 1, B, True)
                last = (l == DEPTH - 1)
                if not last:
                    norm_post(xbi, A, norm_pre(xbi, A), (l + 1) * 8)
                    ff2_unit(gbase, xbi, 7, B, True)
                else:
                    more = tile + 1 < NT
                    if more:
                        load_x_half(tile + 1, A, to_R=True)
                    all_stores += norm_post(xbi, A, norm_pre(xbi, A), 64, final=True, tile=tile)
                    ff2_unit(gbase, xbi, 7, B, True)
                    if more:
                        load_x_half(tile + 1, B, to_R=True)
                        norm_post(xbi, A, norm_pre(xbi, A, True, fromR=True), 0, fromR=True)
                    all_stores += norm_post(xbi, B, norm_pre(xbi, B), 64, final=True, tile=tile)
        assert wstate["next"] == tot_loads

        for e in ("pe", "act", "dve", "pool"):
            n = 0
            for o in S.q[e]:
                if o.is_dma:
                    continue
                if o.needed:
                    n += 1
                    o.ms = n
        for q in ("pool", "sp"):
            for i, o in enumerate(S.dma_ops[q]):
                o.dsem = dsems[q][i % 8]
                o.dval = 16 * (i // 8 + 1)

        def emitter(engname, final_ops=()):
            def body(eng):
                seen = {}

                def wait_for(d):
                    if d.is_dma:
                        sem, val = d.dsem, d.dval
                    else:
                        sem, val = prog[d.eng], d.ms
                    if seen.get(id(sem), 0) >= val:
                        return
                    eng.wait_ge(sem, val)
                    seen[id(sem)] = val
                for o in S.q[engname]:
                    for d in o.deps:
                        wait_for(d)
                    ins = o.fn(eng)
                    if o.is_dma:
                        ins.then_inc(o.dsem, 16)
                    elif o.needed:
                        ins.then_inc(prog[o.eng], 1)
                for d in final_ops:
                    wait_for(d)
            return body

        with nc.Block() as block:
            block.tensor(emitter("pe"))
            block.scalar(emitter("act"))
            block.vector(emitter("dve"))
            block.gpsimd(emitter("pool"))
            block.sync(emitter("sp", all_stores))
    return nc


def _prep_weights(w_in, w_out, w_ff1, w_ff2, depth):
    wst = np.empty((depth, 128, 128, 1024), np.float32)
    for l in range(depth):
        wi = w_in[l].reshape(8, 128, 7, 1024)
        wv = wi[:, :, 4, :].reshape(8, 128, 2, 512)
        wv = wv.transpose(1, 2, 0, 3).reshape(128, 2, 4, 1024)
        wst[l, :, 0:8, :] = wv.reshape(128, 8, 1024)
        gsel = [2, 0, 1, 5, 3, 6]
        wj = wi[:, :, gsel, :].reshape(8, 128, 6, 8, 128)
        wj = wj.transpose(1, 3, 2, 0, 4)
        wst[l, :, 8:56, :] = wj.reshape(128, 48, 1024)
        wo = w_out[l].reshape(8, 128, 8, 128).transpose(1, 2, 0, 3)
        wst[l, :, 56:64, :] = wo.reshape(128, 8, 1024)
        w1 = w_ff1[l].reshape(8, 128, 32, 128).transpose(1, 2, 0, 3)
        wst[l, :, 64:96, :] = w1.reshape(128, 32, 1024)
        w2 = w_ff2[l].reshape(32, 128, 8, 128).transpose(1, 2, 0, 3)
        wst[l, :, 96:128, :] = w2.reshape(128, 32, 1024)
    return wst.reshape(depth, 128, 128 * 1024)


def _prep_small(norm_mix, conv_w, sgu_w, sgu_b, sgu_ln_g, sgu_ln_b, norm_mlp, final_norm, depth):
    pp = np.zeros((128, NPP), np.float32)
    for l in range(depth):
        pp[:, l * 8:(l + 1) * 8] = norm_mix[l].reshape(8, 128).T
        pp[:, 32 + l * 8:32 + (l + 1) * 8] = norm_mlp[l].reshape(8, 128).T
        for k in range(3):
            pp[:, 72 + l * 24 + k * 8:72 + l * 24 + (k + 1) * 8] = conv_w[l, k].reshape(8, 128).T
    pp[:, 64:72] = final_norm.reshape(8, 128).T
    for l in range(depth):
        pp[:, 168 + l * 8:168 + (l + 1) * 8] = sgu_ln_g[l].reshape(8, 128).T
        pp[:, 200 + l * 8:200 + (l + 1) * 8] = sgu_ln_b[l].reshape(8, 128).T
    sgw = np.ascontiguousarray(sgu_w[:depth].transpose(0, 3, 1, 2)).reshape(depth, 128, 1024)
    sgb = np.ascontiguousarray(np.broadcast_to(sgu_b[:depth].reshape(depth, 1, 1024), (depth, 128, 1024)))
    return pp, sgw, sgb


_CFG = dict(DEPTH=4, NT=4, NH=2, RB=20, NXB=1, NTMP=6)
_NC_CACHE = {}


def _run(x, norm_mix, w_in, conv_w, sgu_w, sgu_b, sgu_ln_g, sgu_ln_b, w_out, norm_mlp, w_ff1, w_ff2,
         final_norm, cfg, seq=SEQ, trace=False):
    depth = cfg["DEPTH"]
    key = tuple(sorted(cfg.items()))
    if key not in _NC_CACHE:
        _NC_CACHE[key] = _build(**cfg)
    nc = _NC_CACHE[key]
    f = lambda a: np.asarray(a, dtype=np.float32)
    wst = _prep_weights(f(w_in), f(w_out), f(w_ff1), f(w_ff2), depth)
    pp, sgw, sgb = _prep_small(f(norm_mix), f(conv_w), f(sgu_w), f(sgu_b), f(sgu_ln_g), f(sgu_ln_b),
                                    f(norm_mlp), f(final_norm), depth)
    x = f(x)
    in_maps = []
    for b in range(NCORES):
        xt = x[b, :seq].reshape(seq // HT, HT, 8, 128).transpose(0, 3, 2, 1)
        in_maps.append({"xT": np.ascontiguousarray(xt).reshape(seq // HT, 128, 8 * HT), "wst": wst, "pp": pp,
                        "sgw": sgw, "sgb": sgb})
    res = run_bass_kernel_spmd(nc, in_maps, core_ids=list(range(NCORES)), trace=trace)
    out = np.stack([np.ascontiguousarray(
        res.results[b]["outT"].reshape(seq // HT, 128, 8, HT).transpose(0, 3, 2, 1)).reshape(seq, D)
        for b in range(NCORES)], axis=0)
    return out.astype(np.float32), res


def kernel(x, norm_mix, w_in, conv_w, sgu_w, sgu_b, sgu_ln_g, sgu_ln_b, w_out, norm_mlp, w_ff1, w_ff2,
           final_norm):
    out, _ = _run(x, norm_mix, w_in, conv_w, sgu_w, sgu_b, sgu_ln_g, sgu_ln_b, w_out, norm_mlp, w_ff1,
                  w_ff2, final_norm, _CFG)
    return out
```
